# Optimizing a Trainium2 kernel written in Bass

```python
import jax, jax.numpy as jnp
from jax import lax
import numpy as np

D_MODEL = 1024
BATCH = 2
SEQ = 8192
DEPTH = 2

N_MIXERS = 2
CHUNK = 128
PLE_DIM = 256
LN_EPS = 1e-5
ALPHA = (2 * DEPTH) ** 0.25
BETA = (8 * DEPTH) ** -0.25

GM_HALF = 3 * D_MODEL
GM_GROUPS = 8
GM_GROUP_DIM = GM_HALF // GM_GROUPS

ML_INNER = 2 * D_MODEL
ML_HEADS = 4
ML_HEAD_DIM = ML_INNER // ML_HEADS
ML_CONV = 4
ML_QKV_BLOCK = 4
ML_NBLOCKS = ML_INNER // ML_QKV_BLOCK

D_FF = -(-8 * D_MODEL // (3 * 256)) * 256

N_A = (DEPTH + 1) // 2
N_B = DEPTH // 2

kernel_name = "hybrid_gmlp_mlstm_deepnorm_trunk"


def layer_norm(x, g, b=None):
    xf = x.astype(jnp.float32)
    mu = jnp.mean(xf, axis=-1, keepdims=True)
    var = jnp.mean(jnp.square(xf - mu), axis=-1, keepdims=True)
    y = (xf - mu) * lax.rsqrt(var + LN_EPS) * g.astype(jnp.float32)
    if b is not None:
        y = y + b.astype(jnp.float32)
    return y.astype(x.dtype)


def gmlp_mixer(x, w_in, ln_g, ln_b, w_s, b_s, w_out):
    B, S, _ = x.shape
    nc = S // CHUNK
    z = jax.nn.gelu(x @ w_in)
    u, v = jnp.split(z, 2, axis=-1)
    v = layer_norm(v, ln_g, ln_b)
    v = v.reshape(B, nc, CHUNK, GM_GROUPS, GM_GROUP_DIM)
    causal = jnp.tril(jnp.ones((CHUNK, CHUNK), dtype=bool))
    ws = jnp.where(causal, w_s, jnp.zeros_like(w_s))
    sv = jnp.einsum('gts,bcsgd->bctgd', ws, v) + b_s.T[:, :, None]
    return (u * sv.reshape(B, S, GM_HALF)) @ w_out


def causal_depthwise_conv(x, w, b):
    K, C = w.shape
    y = lax.conv_general_dilated(x, w[:, None, :], window_strides=(1,), padding=[(K - 1, 0)],
                                 dimension_numbers=('NWC', 'WIO', 'NWC'), feature_group_count=C)
    return y + b


def headwise_linear(a, w):
    B, S, _ = a.shape
    a = a.reshape(B, S, ML_NBLOCKS, ML_QKV_BLOCK)
    return jnp.einsum('bsnc,ncd->bsnd', a, w).reshape(B, S, ML_INNER)


def mlstm_cell(q, k, v, i_pre, f_pre):
    out_dtype = v.dtype
    B, S, H, dh = q.shape
    nc = S // CHUNK
    f32 = jnp.float32

    def seq_chunks(a):
        return a.astype(f32).reshape(B, nc, CHUNK, H, dh).transpose(1, 0, 3, 2, 4)

    def gate_chunks(a):
        return a.astype(f32).reshape(B, nc, CHUNK, H).transpose(1, 0, 3, 2)

    qc = seq_chunks(q) * (dh ** -0.5)
    kc = seq_chunks(k)
    vc = seq_chunks(v)
    log_i = gate_chunks(i_pre)
    b_cum = jnp.cumsum(jax.nn.log_sigmoid(gate_chunks(f_pre)), axis=-1)
    causal = jnp.tril(jnp.ones((CHUNK, CHUNK), dtype=bool))

    def step(carry, inp):
        C, n, m = carry
        qb, kb, vb, bb, ib = inp
        dmat = bb[..., :, None] - bb[..., None, :] + ib[..., None, :]
        dmat = jnp.where(causal, dmat, -jnp.inf)
        m_inter = bb + m[..., None]
        m_t = jnp.maximum(m_inter, jnp.max(dmat, axis=-1))
        s = jnp.einsum('bhtd,bhsd->bhts', qb, kb) * jnp.exp(dmat - m_t[..., None])
        scale_inter = jnp.exp(m_inter - m_t)
        num = (jnp.einsum('bhts,bhsd->bhtd', s, vb)
               + scale_inter[..., None] * jnp.einsum('bhtk,bhkv->bhtv', qb, C))
        den = jnp.sum(s, axis=-1) + scale_inter * jnp.einsum('bhtk,bhk->bht', qb, n)
        h = num / jnp.maximum(jnp.abs(den), jnp.exp(-m_t))[..., None]
        b_last = bb[..., -1]
        g = b_last[..., None] - bb + ib
        m_new = jnp.maximum(b_last + m, jnp.max(g, axis=-1))
        wg = jnp.exp(g - m_new[..., None])
        decay = jnp.exp(b_last + m - m_new)
        C = decay[..., None, None] * C + jnp.einsum('bhs,bhsk,bhsv->bhkv', wg, kb, vb)
        n = decay[..., None] * n + jnp.einsum('bhs,bhsk->bhk', wg, kb)
        return (C, n, m_new), h

    init = (jnp.zeros((B, H, dh, dh), f32), jnp.zeros((B, H, dh), f32), jnp.zeros((B, H), f32))
    _, h = lax.scan(step, init, (qc, kc, vc, b_cum, log_i))
    return h.transpose(1, 0, 3, 2, 4).reshape(B, S, H, dh).astype(out_dtype)


def mlstm_mixer(x, w_in, conv_w, conv_b, w_q, w_k, w_v, w_gates, b_gates, skip, norm_g, w_out):
    B, S, _ = x.shape
    xm, z = jnp.split(x @ w_in, 2, axis=-1)
    xc = jax.nn.silu(causal_depthwise_conv(xm, conv_w, conv_b))
    q = headwise_linear(xc, w_q)
    k = headwise_linear(xc, w_k)
    v = headwise_linear(xm, w_v)
    gates = jnp.concatenate([q, k, v], axis=-1) @ w_gates + b_gates
    i_pre, f_pre = gates[..., :ML_HEADS], gates[..., ML_HEADS:]
    split_heads = lambda a: a.reshape(B, S, ML_HEADS, ML_HEAD_DIM)
    h = mlstm_cell(split_heads(q), split_heads(k), split_heads(v), i_pre, f_pre)
    h = layer_norm(h, norm_g).reshape(B, S, ML_INNER) + skip * xc
    return (h * jax.nn.silu(z)) @ w_out


def swiglu(x, w_gate, w_up, w_down):
    return (jax.nn.silu(x @ w_gate) * (x @ w_up)) @ w_down


def setup_inputs(seed: int = 0) -> dict:
    key = jax.random.key(seed)
    ks = iter(jax.random.split(key, 40))
    nrm = lambda shape, scale: jax.random.normal(next(ks), shape, jnp.float32) * scale
    ones_ish = lambda shape: 1.0 + nrm(shape, 0.02)
    b_gates_one = jnp.concatenate([jnp.zeros((ML_HEADS,), jnp.float32),
                                   jnp.linspace(3.0, 6.0, ML_HEADS, dtype=jnp.float32)])
    return {
        "x": nrm((BATCH, SEQ, D_MODEL), 1.0),
        "p": nrm((DEPTH, BATCH, SEQ, PLE_DIM), 1.0),
        "gm_w_in": nrm((N_A, D_MODEL, 2 * GM_HALF), D_MODEL ** -0.5),
        "gm_ln_g": ones_ish((N_A, GM_HALF)),
        "gm_ln_b": nrm((N_A, GM_HALF), 0.02),
        "gm_w_s": nrm((N_A, GM_GROUPS, CHUNK, CHUNK), CHUNK ** -0.5),
        "gm_b_s": ones_ish((N_A, GM_GROUPS, CHUNK)),
        "gm_w_out": nrm((N_A, GM_HALF, D_MODEL), BETA * GM_HALF ** -0.5),
        "ml_w_in": nrm((N_B, D_MODEL, 2 * ML_INNER), D_MODEL ** -0.5),
        "ml_conv_w": nrm((N_B, ML_CONV, ML_INNER), ML_CONV ** -0.5),
        "ml_conv_b": nrm((N_B, ML_INNER), 0.02),
        "ml_w_q": nrm((N_B, ML_NBLOCKS, ML_QKV_BLOCK, ML_QKV_BLOCK), ML_QKV_BLOCK ** -0.5),
        "ml_w_k": nrm((N_B, ML_NBLOCKS, ML_QKV_BLOCK, ML_QKV_BLOCK), ML_QKV_BLOCK ** -0.5),
        "ml_w_v": nrm((N_B, ML_NBLOCKS, ML_QKV_BLOCK, ML_QKV_BLOCK), ML_QKV_BLOCK ** -0.5),
        "ml_w_gates": nrm((N_B, 3 * ML_INNER, 2 * ML_HEADS), (3 * ML_INNER) ** -0.5),
        "ml_b_gates": b_gates_one[None] + nrm((N_B, 2 * ML_HEADS), 0.1),
        "ml_skip": ones_ish((N_B, ML_INNER)),
        "ml_norm_g": ones_ish((N_B, ML_HEADS, ML_HEAD_DIM)),
        "ml_w_out": nrm((N_B, ML_INNER, D_MODEL), BETA * ML_INNER ** -0.5),
        "ln1_g": ones_ish((DEPTH, D_MODEL)),
        "ln1_b": nrm((DEPTH, D_MODEL), 0.02),
        "ln2_g": ones_ish((DEPTH, D_MODEL)),
        "ln2_b": nrm((DEPTH, D_MODEL), 0.02),
        "ffn_w_gate": nrm((DEPTH, D_MODEL, D_FF), D_MODEL ** -0.5),
        "ffn_w_up": nrm((DEPTH, D_MODEL, D_FF), D_MODEL ** -0.5),
        "ffn_w_down": nrm((DEPTH, D_FF, D_MODEL), BETA * D_FF ** -0.5),
        "ple_w_proj": nrm((DEPTH, PLE_DIM, D_MODEL), PLE_DIM ** -0.5),
        "ple_w_gate": nrm((DEPTH, D_MODEL, D_MODEL), D_MODEL ** -0.5),
        "ple_b_gate": nrm((DEPTH, D_MODEL), 0.02),
    }


def reference(x, p, gm_w_in, gm_ln_g, gm_ln_b, gm_w_s, gm_b_s, gm_w_out,
              ml_w_in, ml_conv_w, ml_conv_b, ml_w_q, ml_w_k, ml_w_v, ml_w_gates, ml_b_gates,
              ml_skip, ml_norm_g, ml_w_out, ln1_g, ln1_b, ln2_g, ln2_b,
              ffn_w_gate, ffn_w_up, ffn_w_down, ple_w_proj, ple_w_gate, ple_b_gate):
    for i in range(DEPTH):
        j = i // N_MIXERS
        if i % N_MIXERS == 0:
            y = gmlp_mixer(x, gm_w_in[j], gm_ln_g[j], gm_ln_b[j], gm_w_s[j], gm_b_s[j], gm_w_out[j])
        else:
            y = mlstm_mixer(x, ml_w_in[j], ml_conv_w[j], ml_conv_b[j], ml_w_q[j], ml_w_k[j],
                            ml_w_v[j], ml_w_gates[j], ml_b_gates[j], ml_skip[j], ml_norm_g[j],
                            ml_w_out[j])
        x = layer_norm(ALPHA * x + y, ln1_g[i], ln1_b[i])
        x = layer_norm(ALPHA * x + swiglu(x, ffn_w_gate[i], ffn_w_up[i], ffn_w_down[i]),
                       ln2_g[i], ln2_b[i])
        gate = jax.nn.sigmoid(x @ ple_w_gate[i] + ple_b_gate[i])
        x = x + gate * (p[i] @ ple_w_proj[i])
    return x
```

```python
import numpy as np
from contextlib import ExitStack
import concourse.bass as bass
import concourse.mybir as mybir
from concourse.bass_utils import run_bass_kernel_spmd

F32 = mybir.dt.float32
BF16 = mybir.dt.bfloat16
AF = mybir.ActivationFunctionType
ALU = mybir.AluOpType

ENGS = ("pe", "act", "dve", "pool", "sp")
NCORES = 8
T = 2048
D = 1024
ALPHA = 4 ** 0.25
LN_EPS = 1e-5
GH = 3072
DFF = 2816
PLE = 256


class R:
    __slots__ = ("w", "r")

    def __init__(self):
        self.w = None
        self.r = {}


class Prog:
    def __init__(self, nc, stack, n_dma_sems=32):
        self.nc = nc
        self.q = {e: [] for e in ENGS}
        self.cnt = {e: 0 for e in ENGS}
        self.seen = {e: {} for e in ENGS}
        self.sem = {e: stack.enter_context(nc.semaphore("s_" + e)) for e in ENGS}
        self.dsem = [stack.enter_context(nc.semaphore("d_%d" % i)) for i in range(n_dma_sems)]
        self.dcnt = [0] * n_dma_sems
        self.dnext = 0
        self.out_tokens = []

    def _needs(self, eng, rd, wr):
        need = {}

        def add(tok, same_ok):
            if tok is None:
                return
            k, v = tok
            if k == eng and not same_ok:
                return
            if v > need.get(k, 0):
                need[k] = v
        for t in rd:
            add(t.w, eng != "pe")
        for t in wr:
            add(t.w, False)
            for k, v in t.r.items():
                add((k, v), False)
        waits = []
        sn = self.seen[eng]
        for k, v in need.items():
            if v > sn.get(k, 0):
                sn[k] = v
                waits.append((k, v))
        return waits

    def _semh(self, k):
        return self.sem[k] if isinstance(k, str) else self.dsem[k]

    def op(self, eng, fn, rd=(), wr=()):
        waits = self._needs(eng, rd, wr)
        self.cnt[eng] += 1
        n = self.cnt[eng]
        self.q[eng].append((waits, fn, (eng, 1)))
        for t in rd:
            t.r[eng] = n
        for t in wr:
            t.w = (eng, n)
            t.r = {}

    def dma(self, eng, out, in_, rd=(), wr=(), is_output=False, slow=False):
        k = self.dnext
        self.dnext = (self.dnext + 1) % len(self.dsem)
        waits = self._needs(eng, rd, wr)
        prev = self.dcnt[k]
        sn = self.seen[eng]
        if prev > sn.get(k, 0):
            sn[k] = prev
            waits.append((k, prev))
        self.dcnt[k] += 16
        v = self.dcnt[k]

        def fn(e, out=out, in_=in_):
            if slow:
                return e.dma_start(out=out, in_=in_, allow_slow_non_contiguous=True)
            return e.dma_start(out=out, in_=in_)
        self.q[eng].append((waits, fn, (k, 16)))
        for t in rd:
            t.r[k] = v
        for t in wr:
            t.w = (k, v)
            t.r = {}
        if is_output:
            self.out_tokens.append((k, v))

    def barrier(self):
        for e in ENGS:
            waits = []
            sn = self.seen[e]
            for o in ENGS:
                if o != e and self.cnt[o] > sn.get(o, 0):
                    sn[o] = self.cnt[o]
                    waits.append((o, self.cnt[o]))
            for k, v in enumerate(self.dcnt):
                if v > sn.get(k, 0):
                    sn[k] = v
                    waits.append((k, v))
            if waits:
                self.q[e].append((waits, None, None))

    def finish(self):
        self.barrier()
        nc = self.nc
        with nc.Block() as block:
            def mk(eng):
                def body(e):
                    for waits, fn, inc in self.q[eng]:
                        for wk, wv in waits:
                            e.wait_ge(self._semh(wk), wv)
                        if fn is not None:
                            fn(e).then_inc(self._semh(inc[0]), inc[1])
                return body
            block.sync(mk("sp"))
            block.tensor(mk("pe"))
            block.scalar(mk("act"))
            block.vector(mk("dve"))
            block.gpsimd(mk("pool"))


class Ctx:
    def __init__(self, nc, st):
        self.nc = nc
        self.P = Prog(nc, st)
        self.ps = [st.enter_context(nc.psum_tensor("ps%d" % i, [128, 512], F32)) for i in range(8)]
        self.rps = [R() for _ in range(8)]
        self.pi = 0
        self.uid = 0
        self.ident = st.enter_context(nc.sbuf_tensor("ident", [128, 128], F32))
        self.rid = R()
        P = self.P
        P.op("pool", lambda e: e.memset(self.ident[:], 0.0), wr=[self.rid])
        P.op("pool", lambda e: e.affine_select(out=self.ident[:], in_=self.ident[:], compare_op=ALU.not_equal,
                                               fill=1.0, base=0, pattern=[[-1, 128]], channel_multiplier=1),
             rd=[self.rid], wr=[self.rid])

    def psum(self):
        i = self.pi
        self.pi = (i + 1) % 8
        return self.ps[i], self.rps[i]

    def name(self, s):
        self.uid += 1
        return "%s_%d" % (s, self.uid)


class Ring:
    def __init__(self, C, ph, name, shape, dt, n):
        self.t = [ph.enter_context(C.nc.sbuf_tensor(C.name(name), shape, dt)) for _ in range(n)]
        self.r = [R() for _ in range(n)]
        self.i = 0

    def next(self):
        i = self.i
        self.i = (i + 1) % len(self.t)
        return self.t[i], self.r[i]

    def next4(self, n=6):
        if not hasattr(self, "sub"):
            self.sub = [[R() for _ in range(n)] for _ in self.t]
        i = self.i
        self.i = (i + 1) % len(self.t)
        return self.t[i], self.sub[i]


def tblocks(Tn, w=512):
    out = []
    t0 = 0
    while t0 < Tn:
        out.append((t0, min(w, Tn - t0)))
        t0 += w
    return out


def linear(C, X, W, K, O, Tn, mode, epi, xdt_cast=True):
    nc, P = C.nc, C.P
    KT = K // 128
    OG = min(O, 512)
    with ExitStack() as ph:
        xs = ph.enter_context(nc.sbuf_tensor(C.name("xs"), [128, KT, Tn], BF16))
        rX = [R() for _ in range(KT)]
        Xv = X.rearrange("(kt p) t -> p kt t", p=128)
        for kt in range(KT):
            P.dma("pool", xs[:, kt, :], Xv[:, kt, :], wr=[rX[kt]])
        KG = 4
        nkg = (KT + KG - 1) // KG
        wring = [ph.enter_context(nc.sbuf_tensor(C.name("wsl"), [128, KT, OG], BF16)) for _ in range(2)]
        rW = [[R() for _ in range(nkg)] for _ in range(2)]
        Wv = W.rearrange("(kt p) o -> p kt o", p=128)
        for gi, o0 in enumerate(range(0, O, OG)):
            ow = min(OG, O - o0)
            b = gi % 2
            wsl = wring[b]
            for kg in range(nkg):
                k0, k1 = kg * KG, min(KT, (kg + 1) * KG)
                P.dma("pool", wsl[:, k0:k1, :ow], Wv[:, k0:k1, o0:o0 + ow], wr=[rW[b][kg]])
            if mode == "fm":
                for oi in range(ow // 128):
                    ot = o0 // 128 + oi
                    for (t0, tw) in tblocks(Tn):
                        ps, rps = C.psum()
                        for kt in range(KT):
                            P.op("pe", lambda e, ps=ps, wsl=wsl, kt=kt, oi=oi, t0=t0, tw=tw: e.matmul(
                                ps[:, :tw], lhsT=wsl[:, kt, oi * 128:(oi + 1) * 128], rhs=xs[:, kt, t0:t0 + tw],
                                start=(kt == 0), stop=(kt == KT - 1)),
                                rd=[rW[b][kt // KG], rX[kt]], wr=[rps])
                        epi(ps, rps, ot, t0, tw)
            else:
                for tc in range(Tn // 128):
                    ps, rps = C.psum()
                    for kt in range(KT):
                        P.op("pe", lambda e, ps=ps, wsl=wsl, kt=kt, tc=tc, ow=ow: e.matmul(
                            ps[:, :ow], lhsT=xs[:, kt, tc * 128:(tc + 1) * 128], rhs=wsl[:, kt, :ow],
                            start=(kt == 0), stop=(kt == KT - 1)),
                            rd=[rW[b][kt // KG], rX[kt]], wr=[rps])
                    epi(ps, rps, tc, o0, ow)
        P.barrier()


def load_cols(C, ph, vec, n, name):
    t = ph.enter_context(C.nc.sbuf_tensor(C.name(name), [128, n], F32))
    r = R()
    C.P.dma("sp", t[:], vec.rearrange("(j p) -> p j", p=128), wr=[r], slow=True)
    return t, r


def load_bc(C, ph, vec, n, name):
    t = ph.enter_context(C.nc.sbuf_tensor(C.name(name), [128, n], F32))
    r = R()
    C.P.dma("sp", t[:], vec.partition_broadcast(128), wr=[r])
    return t, r


def ln_tm(C, Xd, gvec, bvec, Dn, Ytm, YT, Tn=T, ytm_dt=F32):
    nc, P = C.nc, C.P
    nst = Dn // 512
    with ExitStack() as ph:
        gb, rg = load_bc(C, ph, gvec, Dn, "lng")
        if bvec is not None:
            bb, rb = load_bc(C, ph, bvec, Dn, "lnb")
        xin = Ring(C, ph, "lnx", [128, Dn], F32, 2)
        yout = Ring(C, ph, "lny", [128, Dn], ytm_dt, 2)
        sts = Ring(C, ph, "lnst", [128, nst, 6], F32, 2)
        mvs = Ring(C, ph, "lnmv", [128, 2], F32, 2)
        rss = Ring(C, ph, "lnrs", [128, 1], F32, 2)
        yts = Ring(C, ph, "lnyt", [128, Dn // 128, 128], BF16, 2) if YT is not None else None
        for c in range(Tn // 128):
            x, rx = xin.next()
            P.dma("sp", x[:], Xd[c * 128:(c + 1) * 128, :], wr=[rx])
            stt, rst = sts.next()
            for j in range(nst):
                P.op("dve", lambda e, stt=stt, x=x, j=j: e.bn_stats(out=stt[:, j, :], in_=x[:, j * 512:(j + 1) * 512]),
                     rd=[rx], wr=[rst])
            mv, rmv = mvs.next()
            P.op("dve", lambda e, mv=mv, stt=stt: e.bn_aggr(out=mv[:], in_=stt[:].rearrange("p a b -> p (a b)")), rd=[rst], wr=[rmv])
            rs, rrs = rss.next()
            P.op("dve", lambda e, rs=rs, mv=mv: e.tensor_scalar_add(out=rs[:], in0=mv[:, 1:2], scalar1=LN_EPS), rd=[rmv], wr=[rrs])
            P.op("act", lambda e, rs=rs: e.activation(out=rs[:], in_=rs[:], func=AF.Sqrt), rd=[rrs], wr=[rrs])
            P.op("dve", lambda e, rs=rs: e.reciprocal(out=rs[:], in_=rs[:]), rd=[rrs], wr=[rrs])
            P.op("dve", lambda e, x=x, mv=mv, rs=rs: e.tensor_scalar(out=x[:], in0=x[:], scalar1=mv[:, 0:1], scalar2=rs[:, 0:1],
                                                                     op0=ALU.subtract, op1=ALU.mult), rd=[rx, rmv, rrs], wr=[rx])
            P.op("pool", lambda e, x=x: e.tensor_tensor(out=x[:], in0=x[:], in1=gb[:], op=ALU.mult), rd=[rx, rg], wr=[rx])
            y, ry = yout.next()
            if bvec is None:
                ysrc, rysrc = x, rx
            elif ytm_dt == F32:
                P.op("pool", lambda e, x=x, y=y: e.tensor_tensor(out=y[:], in0=x[:], in1=bb[:], op=ALU.add), rd=[rx, rb], wr=[ry])
                ysrc, rysrc = y, ry
            else:
                P.op("pool", lambda e, x=x: e.tensor_tensor(out=x[:], in0=x[:], in1=bb[:], op=ALU.add), rd=[rx, rb], wr=[rx])
                P.op("act", lambda e, x=x, y=y: e.activation(out=y[:], in_=x[:], func=AF.Copy), rd=[rx], wr=[ry])
                ysrc, rysrc = x, rx
            if Ytm is not None:
                P.dma("sp", Ytm[c * 128:(c + 1) * 128, :], y[:], rd=[ry])
            if YT is not None:
                yt, ryt = yts.next()
                for j0 in range(0, Dn // 128, 4):
                    ps, rps = C.psum()
                    for j in range(j0, min(j0 + 4, Dn // 128)):
                        P.op("pe", lambda e, ps=ps, ysrc=ysrc, j=j, j0=j0: e.transpose(
                            ps[:, (j - j0) * 128:(j - j0 + 1) * 128], ysrc[:, j * 128:(j + 1) * 128], C.ident[:]),
                            rd=[rysrc, C.rid], wr=[rps])
                    nj = min(4, Dn // 128 - j0)
                    P.op("act", lambda e, ps=ps, yt=yt, j0=j0, nj=nj: e.activation(
                        out=yt[:, j0:j0 + nj, :], in_=ps[:, :nj * 128].rearrange("p (j t) -> p j t", t=128), func=AF.Copy),
                        rd=[rps], wr=[ryt])
                YTv = YT.rearrange("(j p) t -> p j t", p=128)
                for j0 in range(0, Dn // 128, 4):
                    P.dma("sp", YTv[:, j0:j0 + 4, c * 128:(c + 1) * 128], yt[:, j0:j0 + 4, :], rd=[ryt])
        P.barrier()


def spatial(C, Vn, U, ws, bs, G, Tn=T):
    nc, P = C.nc, C.P
    NG, FT = 8, GH // 128
    with ExitStack() as ph:
        wsT = ph.enter_context(nc.sbuf_tensor(C.name("wsT"), [128, NG, 128], BF16))
        rwsT = R()
        wtmp = Ring(C, ph, "wtmp", [128, 128], F32, 2)
        for g in range(NG):
            wt, rwt = wtmp.next()
            P.dma("sp", wt[:], ws[g], wr=[rwt])
            P.op("pool", lambda e, wt=wt: e.affine_select(out=wt[:], in_=wt[:], compare_op=ALU.is_ge, fill=0.0, base=0,
                                                          pattern=[[-1, 128]], channel_multiplier=1), rd=[rwt], wr=[rwt])
            ps, rps = C.psum()
            P.op("pe", lambda e, ps=ps, wt=wt: e.transpose(ps[:, :128], wt[:], C.ident[:]), rd=[rwt, C.rid], wr=[rps])
            P.op("act", lambda e, ps=ps, g=g: e.activation(out=wsT[:, g, :], in_=ps[:, :128], func=AF.Copy), rd=[rps], wr=[rwsT])
        bsb = ph.enter_context(nc.sbuf_tensor(C.name("bsb"), [128, NG * 128], F32))
        rbsb = R()
        P.dma("sp", bsb[:], bs.rearrange("g t -> (g t)").partition_broadcast(128), wr=[rbsb])
        vring = Ring(C, ph, "vnb", [128, 4, GH], BF16, 2)
        uring = Ring(C, ph, "ub", [128, FT, 512], BF16, 2)
        gring = Ring(C, ph, "gb", [128, FT, 512], BF16, 2)
        tmpr = Ring(C, ph, "sptmp", [128, 512], F32, 3)
        Uv = U.rearrange("(ft p) t -> p ft t", p=128)
        Gv = G.rearrange("(ft p) t -> p ft t", p=128)
        for (t0, tw) in tblocks(Tn):
            vn, rvnl = vring.next4()
            for cc in range(4):
                P.dma("sp", vn[:, cc, :], Vn[t0 + cc * 128:t0 + (cc + 1) * 128, :], wr=[rvnl[cc]])
            ub, rubl = uring.next4()
            for q4 in range(FT // 4):
                P.dma("sp", ub[:, q4 * 4:(q4 + 1) * 4, :], Uv[:, q4 * 4:(q4 + 1) * 4, t0:t0 + 512], wr=[rubl[q4]])
            gb, rgbl = gring.next4()
            for ft in range(FT):
                g = ft // 3
                ps, rps = C.psum()
                for cc in range(4):
                    P.op("pe", lambda e, ps=ps, vn=vn, cc=cc, ft=ft, g=g: e.matmul(
                        ps[:, cc * 128:(cc + 1) * 128], lhsT=vn[:, cc, ft * 128:(ft + 1) * 128], rhs=wsT[:, g, :],
                        start=True, stop=True), rd=[rvnl[cc], rwsT], wr=[rps])
                tmp, rtmp = tmpr.next()
                P.op("dve", lambda e, ps=ps, tmp=tmp, g=g: e.tensor_tensor(
                    out=tmp[:].rearrange("p (c t) -> p c t", t=128), in0=ps[:].rearrange("p (c t) -> p c t", t=128),
                    in1=bsb[:, g * 128:(g + 1) * 128].unsqueeze(1).to_broadcast([128, 4, 128]), op=ALU.add),
                    rd=[rps, rbsb], wr=[rtmp])
                P.op("dve", lambda e, tmp=tmp, gb=gb, ub=ub, ft=ft: e.tensor_tensor(
                    out=gb[:, ft, :], in0=tmp[:], in1=ub[:, ft, :], op=ALU.mult), rd=[rtmp, rubl[ft // 4]], wr=[rgbl[ft // 4]])
            for q4 in range(FT // 4):
                P.dma("sp", Gv[:, q4 * 4:(q4 + 1) * 4, t0:t0 + 512], gb[:, q4 * 4:(q4 + 1) * 4, :], rd=[rgbl[q4]])
        P.barrier()


GELU = AF.Gelu


def layer_tail(C, ph0, Xtm_in, Gsrc, Kmix, w_mix_out, ln1g, ln1b, wg, wu, wd, ln2g, ln2b, pwg, pbg, pT, pwp, Out, scr):
    nc, P = C.nc, C.P
    R1, X1tm, X1T, Hg, H, R2, X2tm, X2T, Gt = scr
    with ExitStack() as ph:
        xr = Ring(C, ph, "xres", [128, 512], F32, 3)
        yr = Ring(C, ph, "yres", [128, 512], F32, 3)

        def epi(ps, rps, tc, o0, ow):
            x, rx = xr.next()
            P.dma("sp", x[:, :ow], Xtm_in[tc * 128:(tc + 1) * 128, o0:o0 + ow], wr=[rx])
            y, ry = yr.next()
            P.op("dve", lambda e: e.scalar_tensor_tensor(out=y[:, :ow], in0=x[:, :ow], scalar=ALPHA, in1=ps[:, :ow],
                                                         op0=ALU.mult, op1=ALU.add), rd=[rx, rps], wr=[ry])
            P.dma("sp", R1[tc * 128:(tc + 1) * 128, o0:o0 + ow], y[:, :ow], rd=[ry])
        linear(C, Gsrc, w_mix_out, Kmix, D, T, "tm", epi)
    ln_tm(C, R1, ln1g, ln1b, D, X1tm, X1T)
    with ExitStack() as ph:
        yr = Ring(C, ph, "hgo", [128, 512], BF16, 3)

        def epi(ps, rps, ot, t0, tw):
            y, ry = yr.next()
            P.op("act", lambda e: e.activation(out=y[:, :tw], in_=ps[:, :tw], func=AF.Silu), rd=[rps], wr=[ry])
            P.dma("sp", Hg[ot * 128:(ot + 1) * 128, t0:t0 + tw], y[:, :tw], rd=[ry])
        linear(C, X1T, wg, D, DFF, T, "fm", epi)
    with ExitStack() as ph:
        mr = Ring(C, ph, "hgi", [128, 512], BF16, 3)
        yr = Ring(C, ph, "ho", [128, 512], BF16, 3)

        def epi(ps, rps, ot, t0, tw):
            m, rm = mr.next()
            P.dma("sp", m[:, :tw], Hg[ot * 128:(ot + 1) * 128, t0:t0 + tw], wr=[rm])
            y, ry = yr.next()
            P.op("dve", lambda e: e.tensor_tensor(out=y[:, :tw], in0=ps[:, :tw], in1=m[:, :tw], op=ALU.mult),
                 rd=[rps, rm], wr=[ry])
            P.dma("sp", H[ot * 128:(ot + 1) * 128, t0:t0 + tw], y[:, :tw], rd=[ry])
        linear(C, X1T, wu, D, DFF, T, "fm", epi)
    with ExitStack() as ph:
        xr = Ring(C, ph, "xres2", [128, 512], F32, 3)
        yr = Ring(C, ph, "yres2", [128, 512], F32, 3)

        def epi(ps, rps, tc, o0, ow):
            x, rx = xr.next()
            P.dma("sp", x[:, :ow], X1tm[tc * 128:(tc + 1) * 128, o0:o0 + ow], wr=[rx])
            y, ry = yr.next()
            P.op("dve", lambda e: e.scalar_tensor_tensor(out=y[:, :ow], in0=x[:, :ow], scalar=ALPHA, in1=ps[:, :ow],
                                                         op0=ALU.mult, op1=ALU.add), rd=[rx, rps], wr=[ry])
            P.dma("sp", R2[tc * 128:(tc + 1) * 128, o0:o0 + ow], y[:, :ow], rd=[ry])
        linear(C, H, wd, DFF, D, T, "tm", epi)
    ln_tm(C, R2, ln2g, ln2b, D, X2tm, X2T)
    with ExitStack() as ph:
        bgb, rbgb = load_bc(C, ph, pbg, D, "pbg")
        tr = Ring(C, ph, "gtt", [128, 512], F32, 3)
        yr = Ring(C, ph, "gto", [128, 512], F32, 3)

        def epi(ps, rps, tc, o0, ow):
            t, rt = tr.next()
            P.op("dve", lambda e: e.tensor_tensor(out=t[:, :ow], in0=ps[:, :ow], in1=bgb[:, o0:o0 + ow], op=ALU.add),
                 rd=[rps, rbgb], wr=[rt])
            y, ry = yr.next()
            P.op("act", lambda e: e.activation(out=y[:, :ow], in_=t[:, :ow], func=AF.Sigmoid), rd=[rt], wr=[ry])
            P.dma("sp", Gt[tc * 128:(tc + 1) * 128, o0:o0 + ow], y[:, :ow], rd=[ry])
        linear(C, X2T, pwg, D, D, T, "tm", epi)
    with ExitStack() as ph:
        xr = Ring(C, ph, "x2i", [128, 512], F32, 3)
        gr = Ring(C, ph, "gti", [128, 512], F32, 3)
        yr = Ring(C, ph, "x3o", [128, 512], F32, 3)

        def epi(ps, rps, tc, o0, ow):
            x, rx = xr.next()
            P.dma("sp", x[:, :ow], X2tm[tc * 128:(tc + 1) * 128, o0:o0 + ow], wr=[rx])
            g, rg = gr.next()
            P.dma("sp", g[:, :ow], Gt[tc * 128:(tc + 1) * 128, o0:o0 + ow], wr=[rg])
            y, ry = yr.next()
            P.op("dve", lambda e: e.tensor_tensor(out=y[:, :ow], in0=ps[:, :ow], in1=g[:, :ow], op=ALU.mult),
                 rd=[rps, rg], wr=[ry])
            P.op("pool", lambda e: e.tensor_tensor(out=y[:, :ow], in0=y[:, :ow], in1=x[:, :ow], op=ALU.add),
                 rd=[ry, rx], wr=[ry])
            P.dma("sp", Out[tc * 128:(tc + 1) * 128, o0:o0 + ow], y[:, :ow], rd=[ry], is_output=True)
        linear(C, pT, pwp, PLE, D, T, "tm", epi)


def tail_scratch(nc):
    dt = lambda n, s, d: nc.dram_tensor(n, s, d).ap()
    return (dt("R1", [T, D], F32), dt("X1tm", [T, D], F32), dt("X1T", [D, T], BF16), dt("Hg", [DFF, T], BF16),
            dt("H", [DFF, T], BF16), dt("R2", [T, D], F32), dt("X2tm", [T, D], F32), dt("X2T", [D, T], BF16),
            dt("Gt", [T, D], F32))


def build_l0():
    nc = bass.Bass("TRN2", target_bir_lowering=False)
    din = lambda n, s: nc.dram_tensor(n, s, F32, kind="ExternalInput").ap()
    xT = din("xT", [D, T]); xtm = din("xtm", [T, D]); pT = din("pT", [PLE, T])
    w_in = din("gm_w_in", [D, 2 * GH]); lng = din("gm_ln_g", [GH]); lnb = din("gm_ln_b", [GH])
    ws = din("gm_w_s", [8, 128, 128]); bs = din("gm_b_s", [8, 128]); w_out = din("gm_w_out", [GH, D])
    ln1g = din("ln1_g", [D]); ln1b = din("ln1_b", [D]); ln2g = din("ln2_g", [D]); ln2b = din("ln2_b", [D])
    wg = din("ffn_w_gate", [D, DFF]); wu = din("ffn_w_up", [D, DFF]); wd = din("ffn_w_down", [DFF, D])
    pwp = din("ple_w_proj", [PLE, D]); pwg = din("ple_w_gate", [D, D]); pbg = din("ple_b_gate", [D])
    out = nc.dram_tensor("out", [T, D], F32, kind="ExternalOutput").ap()
    dt = lambda n, s, d: nc.dram_tensor(n, s, d).ap()
    U = dt("U", [GH, T], BF16); V = dt("V", [T, GH], F32); Vn = dt("Vn", [T, GH], BF16); G = dt("G", [GH, T], BF16)
    scr = tail_scratch(nc)
    with ExitStack() as st:
        C = Ctx(nc, st)
        P = C.P
        with ExitStack() as ph:
            yr = Ring(C, ph, "uo", [128, 512], BF16, 3)

            def epi(ps, rps, ot, t0, tw):
                y, ry = yr.next()
                P.op("act", lambda e: e.activation(out=y[:, :tw], in_=ps[:, :tw], func=GELU), rd=[rps], wr=[ry])
                P.dma("sp", U[ot * 128:(ot + 1) * 128, t0:t0 + tw], y[:, :tw], rd=[ry])
            linear(C, xT, w_in[:, 0:GH], D, GH, T, "fm", epi)
        with ExitStack() as ph:
            yr = Ring(C, ph, "vo", [128, 512], F32, 3)

            def epi(ps, rps, tc, o0, ow):
                y, ry = yr.next()
                P.op("act", lambda e: e.activation(out=y[:, :ow], in_=ps[:, :ow], func=GELU), rd=[rps], wr=[ry])
                P.dma("sp", V[tc * 128:(tc + 1) * 128, o0:o0 + ow], y[:, :ow], rd=[ry])
            linear(C, xT, w_in[:, GH:2 * GH], D, GH, T, "tm", epi)
        ln_tm(C, V, lng, lnb, GH, Vn, None, ytm_dt=BF16)
        spatial(C, Vn, U, ws, bs, G)
        layer_tail(C, None, xtm, G, GH, w_out, ln1g, ln1b, wg, wu, wd, ln2g, ln2b, pwg, pbg, pT, pwp, out, scr)
        P.finish()
    return nc


def l0_inputs(inp, c):
    b, s0 = c // 4, (c % 4) * T
    xs = np.ascontiguousarray(inp["x"][b, s0:s0 + T])
    m = {"xT": np.ascontiguousarray(xs.T), "xtm": xs, "pT": np.ascontiguousarray(inp["p"][0, b, s0:s0 + T].T)}
    for k in ("gm_w_in", "gm_ln_g", "gm_ln_b", "gm_w_s", "gm_b_s", "gm_w_out"):
        m[k] = np.ascontiguousarray(inp[k][0])
    for k in ("ln1_g", "ln1_b", "ln2_g", "ln2_b", "ffn_w_gate", "ffn_w_up", "ffn_w_down", "ple_w_proj", "ple_w_gate", "ple_b_gate"):
        m[k] = np.ascontiguousarray(inp[k][0])
    return m


def run_l0(inp):
    nc = build_l0()
    res = run_bass_kernel_spmd(nc, [l0_inputs(inp, c) for c in range(NCORES)], core_ids=list(range(NCORES)))
    return [r["out"] for r in res.results]


MI = 2048
NH = 4
DH = 512
TE = T + 128
NCH = T // 128
NEG = -1.0e30


def ml_front(C, x1Te, w_in, convw, convb, wq, wk, wv, wgt, bgt, scr):
    nc, P = C.nc, C.P
    XM, XC, XMo, QKVT, Ktm, Vtm, GTfm = scr
    with ExitStack() as ph:
        yr = Ring(C, ph, "xmo", [128, 512], F32, 3)

        def epi(ps, rps, ot, t0, tw):
            y, ry = yr.next()
            P.op("act", lambda e: e.activation(out=y[:, :tw], in_=ps[:, :tw], func=AF.Copy), rd=[rps], wr=[ry])
            P.dma("sp", XM[ot * 128:(ot + 1) * 128, t0:t0 + tw], y[:, :tw], rd=[ry])
        linear(C, x1Te, w_in[:, 0:MI], D, MI, TE, "fm", epi)
    with ExitStack() as ph:
        cw = ph.enter_context(nc.sbuf_tensor(C.name("cw"), [128, 16, 4], F32)); rcw = R()
        cb = ph.enter_context(nc.sbuf_tensor(C.name("cb"), [128, 16], F32)); rcb = R()
        P.dma("sp", cw[:], convw, wr=[rcw])
        P.dma("sp", cb[:], convb, wr=[rcb])
        xin = Ring(C, ph, "cxm", [128, TE], F32, 2)
        acc = Ring(C, ph, "cacc", [128, T], F32, 2)
        xco = Ring(C, ph, "cxc", [128, T], BF16, 2)
        xmo = Ring(C, ph, "cxo", [128, T], BF16, 2)
        for j in range(16):
            x, rx = xin.next()
            P.dma("sp", x[:], XM[j * 128:(j + 1) * 128, :], wr=[rx])
            a, ra = acc.next()
            P.op("dve", lambda e, a=a, x=x, j=j: e.tensor_scalar_mul(out=a[:], in0=x[:, 125:125 + T], scalar1=cw[:, j, 0:1]),
                 rd=[rx, rcw], wr=[ra])
            for k in range(1, 4):
                P.op("dve", lambda e, a=a, x=x, j=j, k=k: e.scalar_tensor_tensor(
                    out=a[:], in0=x[:, 125 + k:125 + k + T], scalar=cw[:, j, k:k + 1], in1=a[:], op0=ALU.mult, op1=ALU.add),
                    rd=[rx, rcw, ra], wr=[ra])
            xc, rxc = xco.next()
            P.op("act", lambda e, a=a, xc=xc, j=j: e.activation(out=xc[:], in_=a[:], func=AF.Silu, bias=cb[:, j:j + 1]),
                 rd=[ra, rcb], wr=[rxc])
            P.dma("sp", XC[j * 128:(j + 1) * 128, :], xc[:], rd=[rxc])
            xo, rxo = xmo.next()
            P.op("pool", lambda e, xo=xo, x=x: e.tensor_copy(out=xo[:], in_=x[:, 128:128 + T]), rd=[rx], wr=[rxo])
            P.dma("sp", XMo[j * 128:(j + 1) * 128, :], xo[:], rd=[rxo])
        P.barrier()
    with ExitStack() as ph:
        wbd = ph.enter_context(nc.sbuf_tensor(C.name("wbd"), [128, 3, 16, 128], BF16)); rwbd = [R() for _ in range(3)]
        for i, w in enumerate((wq, wk, wv)):
            P.dma("pool", wbd[:, i], w.rearrange("j p o -> p j o"), wr=[rwbd[i]])
        xcr = Ring(C, ph, "hxc", [128, T], BF16, 2)
        xmr = Ring(C, ph, "hxm", [128, T], BF16, 2)
        fo = Ring(C, ph, "hfo", [128, 512], BF16, 4)
        to = Ring(C, ph, "hto", [128, 4, 128], BF16, 4)
        Kv = Ktm.rearrange("(c p) ch -> p c ch", p=128)
        Vv = Vtm.rearrange("(c p) ch -> p c ch", p=128)
        for j in range(16):
            xc, rxc = xcr.next()
            P.dma("sp", xc[:], XC[j * 128:(j + 1) * 128, :], wr=[rxc])
            xm, rxm = xmr.next()
            P.dma("sp", xm[:], XMo[j * 128:(j + 1) * 128, :], wr=[rxm])
            for i, (src, rsrc) in enumerate(((xc, rxc), (xc, rxc), (xm, rxm))):
                for (t0, tw) in tblocks(T):
                    ps, rps = C.psum()
                    P.op("pe", lambda e, ps=ps, i=i, j=j, src=src, t0=t0: e.matmul(
                        ps[:, :512], lhsT=wbd[:, i, j, :], rhs=src[:, t0:t0 + 512], start=True, stop=True),
                        rd=[rwbd[i], rsrc], wr=[rps])
                    y, ry = fo.next()
                    P.op("act", lambda e, ps=ps, y=y: e.activation(out=y[:], in_=ps[:], func=AF.Copy), rd=[rps], wr=[ry])
                    P.dma("sp", QKVT[i * MI + j * 128:i * MI + (j + 1) * 128, t0:t0 + 512], y[:], rd=[ry])
            for i, (src, rsrc, dst) in ((1, (xc, rxc, Kv)), (2, (xm, rxm, Vv))):
                for cb4 in range(NCH // 4):
                    ps, rps = C.psum()
                    for cc in range(4):
                        c = cb4 * 4 + cc
                        P.op("pe", lambda e, ps=ps, i=i, j=j, src=src, c=c, cc=cc: e.matmul(
                            ps[:, cc * 128:(cc + 1) * 128], lhsT=src[:, c * 128:(c + 1) * 128], rhs=wbd[:, i, j, :],
                            start=True, stop=True), rd=[rwbd[i], rsrc], wr=[rps])
                    y, ry = to.next()
                    P.op("dve", lambda e, ps=ps, y=y: e.tensor_copy(out=y[:], in_=ps[:].rearrange("p (c t) -> p c t", t=128)),
                         rd=[rps], wr=[ry])
                    P.dma("sp", dst[:, cb4 * 4:cb4 * 4 + 4, j * 128:(j + 1) * 128], y[:], rd=[ry])
        P.barrier()
    with ExitStack() as ph:
        KT = 3 * MI // 128
        wg = ph.enter_context(nc.sbuf_tensor(C.name("wgt"), [128, KT, 8], BF16)); rwg = R()
        P.dma("pool", wg[:], wgt.rearrange("(kt p) o -> p kt o", p=128), wr=[rwg])
        bg = ph.enter_context(nc.sbuf_tensor(C.name("bgt"), [8, 1], F32)); rbg = R()
        P.dma("sp", bg[:], bgt, wr=[rbg])
        xr = Ring(C, ph, "gx", [128, KT, 512], BF16, 2)
        yr = Ring(C, ph, "gy", [8, 512], F32, 2)
        Xv = QKVT.rearrange("(kt p) t -> p kt t", p=128)
        for (t0, tw) in tblocks(T):
            x, rxl = xr.next4()
            for k8 in range(KT // 8):
                P.dma("sp", x[:, k8 * 8:(k8 + 1) * 8, :], Xv[:, k8 * 8:(k8 + 1) * 8, t0:t0 + 512], wr=[rxl[k8]])
            ps, rps = C.psum()
            for kt in range(KT):
                P.op("pe", lambda e, ps=ps, x=x, kt=kt: e.matmul(ps[0:8, :], lhsT=wg[:, kt, :], rhs=x[:, kt, :],
                                                                 start=(kt == 0), stop=(kt == KT - 1)), rd=[rwg, rxl[kt // 8]], wr=[rps])
            y, ry = yr.next()
            P.op("act", lambda e, ps=ps, y=y: e.activation(out=y[:], in_=ps[0:8, :], func=AF.Identity, bias=bg[:, 0:1]),
                 rd=[rps, rbg], wr=[ry])
            P.dma("sp", GTfm[:, t0:t0 + 512], y[:], rd=[ry])
        P.barrier()


def ml_cell(C, scr, m0_bc, Cinit, ninit, with_out, Htm, outs):
    nc, P = C.nc, C.P
    XM, XC, XMo, QKVT, Ktm, Vtm, GTfm = scr
    Ct, rC = Cinit
    nt, rn = ninit
    m0, rm0 = m0_bc
    NR = NH * NCH
    with ExitStack() as ph:
        sb = lambda name, shape, dt=F32: ph.enter_context(nc.sbuf_tensor(C.name(name), shape, dt))
        ic = sb("ic", [NR, 128]); fc = sb("fc", [NR, 128]); rg = R()
        Gv = GTfm.rearrange("g (c t) -> g c t", t=128)
        rgl = [R() for _ in range(8)]
        for h in range(NH):
            P.dma("sp", ic[h * NCH:(h + 1) * NCH, :], Gv[h], wr=[rgl[h]])
            P.dma("sp", fc[h * NCH:(h + 1) * NCH, :], Gv[NH + h], wr=[rgl[4 + h]])
        t1 = sb("t1", [NR, 128]); t2 = sb("t2", [NR, 128]); lf = sb("lf", [NR, 128])
        r1, r2, rlf = R(), R(), R()
        P.op("act", lambda e: e.activation(out=t1[:], in_=fc[:], func=AF.Abs), rd=rgl[4:], wr=[r1])
        P.op("act", lambda e: e.activation(out=t1[:], in_=t1[:], func=AF.Exp, scale=-1.0), rd=[r1], wr=[r1])
        P.op("act", lambda e: e.activation(out=t1[:], in_=t1[:], func=AF.Ln, bias=1.0), rd=[r1], wr=[r1])
        P.op("dve", lambda e: e.tensor_scalar_min(out=t2[:], in0=fc[:], scalar1=0.0), rd=rgl[4:], wr=[r2])
        P.op("dve", lambda e: e.tensor_sub(out=lf[:], in0=t2[:], in1=t1[:]), rd=[r1, r2], wr=[rlf])

        def scan(src, rsrc, op, name):
            a, ra = src, rsrc
            bufs = [(sb(name + "a", [NR, 128]), R()), (sb(name + "b", [NR, 128]), R())]
            d = 1
            i = 0
            while d < 128:
                o, ro = bufs[i % 2]
                P.op("dve", lambda e, o=o, a=a, d=d: e.tensor_copy(out=o[:, 0:d], in_=a[:, 0:d]), rd=[ra], wr=[ro])
                P.op("dve", lambda e, o=o, a=a, d=d: e.tensor_tensor(out=o[:, d:128], in0=a[:, d:128], in1=a[:, 0:128 - d], op=op),
                     rd=[ra, ro], wr=[ro])
                a, ra = o, ro
                d *= 2
                i += 1
            return a, ra
        bcs, rb = scan(lf, rlf, ALU.add, "bs")
        av = sb("av", [NR, 128]); rav = R()
        P.op("dve", lambda e: e.tensor_sub(out=av[:], in0=ic[:], in1=bcs[:]), rd=rgl[:4] + [rb], wr=[rav])
        cmx, rcm = scan(av, rav, ALU.max, "cm")
        rows = sb("rows", [1, 2, NR]); rrows = R()
        for i, (src, rsrc) in enumerate(((bcs, rb), (cmx, rcm))):
            ps, rps = C.psum()
            P.op("pe", lambda e, ps=ps, src=src: e.transpose(ps[0:1, 0:NR], src[:, 127:128], C.ident[0:NR, 0:NR]),
                 rd=[rsrc, C.rid], wr=[rps])
            P.op("dve", lambda e, ps=ps, i=i: e.tensor_copy(out=rows[:, i, :], in_=ps[0:1, 0:NR]), rd=[rps], wr=[rrows])
        mp = sb("mp", [1, NH, NCH + 1]); ml = sb("ml", [1, NH, NCH]); rmp = R(); rml = R()
        blr = rows[:, 0, :].rearrange("p (h c) -> p h c", c=NCH)
        clr = rows[:, 1, :].rearrange("p (h c) -> p h c", c=NCH)
        P.op("dve", lambda e: e.tensor_copy(out=mp[:, :, 0], in_=m0[0:1, :]), rd=[rm0], wr=[rmp])
        for c in range(NCH):
            P.op("dve", lambda e, c=c: e.tensor_tensor(out=ml[:, :, c], in0=mp[:, :, c], in1=clr[:, :, c], op=ALU.max),
                 rd=[rmp, rrows], wr=[rml])
            P.op("dve", lambda e, c=c: e.tensor_tensor(out=mp[:, :, c + 1], in0=ml[:, :, c], in1=blr[:, :, c], op=ALU.add),
                 rd=[rml, rrows, rmp], wr=[rmp])
        fg = sb("fg", [1, 2, NH]); rfg = R()
        P.op("dve", lambda e: e.tensor_reduce(out=fg[:, 0, :], in_=blr, axis=mybir.AxisListType.X, op=ALU.add), rd=[rrows], wr=[rfg])
        P.op("dve", lambda e: e.tensor_copy(out=fg[:, 1, :], in_=mp[:, :, NCH]), rd=[rmp, rfg], wr=[rfg])
        cols = sb("cols", [NR, 2]); rcols = R()
        mpc = sb("mpc", [1, NR]); rmpc = R()
        P.op("dve", lambda e: e.tensor_copy(out=mpc[:].rearrange("p (h c) -> p h c", c=NCH), in_=mp[:, :, 0:NCH]), rd=[rmp], wr=[rmpc])
        for i, (src, rsrc) in enumerate(((mpc[:], rmpc), (ml[:].rearrange("p h c -> p (h c)"), rml))):
            ps, rps = C.psum()
            P.op("pe", lambda e, ps=ps, src=src: e.transpose(ps[0:NR, 0:1], src, C.ident[0:1, 0:1]), rd=[rsrc, C.rid], wr=[rps])
            P.op("dve", lambda e, ps=ps, i=i: e.tensor_copy(out=cols[:, i:i + 1], in_=ps[0:NR, 0:1]), rd=[rps], wr=[rcols])
        ncols = sb("ncols", [NR, 2]); rnc = R()
        P.op("dve", lambda e: e.tensor_scalar_mul(out=ncols[:], in0=cols[:], scalar1=-1.0), rd=[rcols], wr=[rnc])
        negM = sb("negM", [NR, 128]); rnM = R()
        P.op("dve", lambda e: e.tensor_scalar(out=negM[:], in0=cmx[:], scalar1=cols[:, 0:1], scalar2=-1.0, op0=ALU.max, op1=ALU.mult),
             rd=[rcm, rcols], wr=[rnM])
        em = sb("em", [NR, 128]); rem = R()
        P.op("dve", lambda e: e.tensor_sub(out=em[:], in0=negM[:], in1=bcs[:]), rd=[rnM, rb], wr=[rem])
        P.op("act", lambda e: e.activation(out=em[:], in_=em[:], func=AF.Exp), rd=[rem], wr=[rem])
        wgc = sb("wgc", [NR, 128]); rwgc = R()
        P.op("act", lambda e: e.activation(out=wgc[:], in_=av[:], func=AF.Exp, bias=ncols[:, 1:2]), rd=[rav, rnc], wr=[rwgc])
        dec = sb("dec", [NR, 1]); rdec = R()
        P.op("act", lambda e: e.activation(out=dec[:], in_=cols[:, 0:1], func=AF.Exp, bias=ncols[:, 1:2]), rd=[rcols, rnc], wr=[rdec])
        tmv = sb("tmv", [128, 3, NR]); rtmv = R()
        for i, (src, rsrc) in enumerate(((av, rav), (wgc, rwgc), (em, rem))):
            ps, rps = C.psum()
            P.op("pe", lambda e, ps=ps, src=src: e.transpose(ps[:, 0:NR], src[:], C.ident[0:NR, 0:NR]), rd=[rsrc, C.rid], wr=[rps])
            P.op("dve", lambda e, ps=ps, i=i: e.tensor_copy(out=tmv[:, i, :], in_=ps[:, 0:NR]), rd=[rps], wr=[rtmv])
        def hilo(src_ap, rsrc, shape, name):
            hi = sb(name + "h", shape, BF16); lo = sb(name + "l", shape, BF16); tf = sb(name + "f", shape); rr = R()
            P.op("dve", lambda e: e.tensor_copy(out=hi[:], in_=src_ap), rd=[rsrc], wr=[rr])
            P.op("dve", lambda e: e.tensor_copy(out=tf[:], in_=hi[:]), rd=[rr], wr=[rr])
            P.op("dve", lambda e: e.tensor_sub(out=tf[:], in0=src_ap, in1=tf[:]), rd=[rsrc, rr], wr=[rr])
            P.op("dve", lambda e: e.tensor_copy(out=lo[:], in_=tf[:]), rd=[rr], wr=[rr])
            return hi, lo, rr
        onesb = sb("onesb", [128, 128], BF16); rones = R()
        P.op("pool", lambda e: e.memset(onesb[:], 1.0), wr=[rones])
        bc = sb("bc", [128, 2, NR]); rbc = R()
        for i, (src, rsrc) in enumerate(((dec[:, 0:1], rdec), (cols[:, 0:1], rcols))):
            dg = sb("dg%d" % i, [NR, NR]); rdg = R()
            P.op("dve", lambda e, dg=dg, src=src: e.tensor_scalar_mul(out=dg[:], in0=C.ident[0:NR, 0:NR], scalar1=src), rd=[rsrc, C.rid], wr=[rdg])
            hi, lo, rhl = hilo(dg[:], rdg, [NR, NR], "dg%d" % i)
            ps, rps = C.psum()
            P.op("pe", lambda e, ps=ps, hi=hi: e.matmul(ps[:, 0:NR], lhsT=onesb[0:NR, :], rhs=hi[:], start=True, stop=False), rd=[rones, rhl], wr=[rps])
            P.op("pe", lambda e, ps=ps, lo=lo: e.matmul(ps[:, 0:NR], lhsT=onesb[0:NR, :], rhs=lo[:], start=False, stop=True), rd=[rones, rhl], wr=[rps])
            P.op("dve", lambda e, ps=ps, i=i: e.tensor_copy(out=bc[:, i, :], in_=ps[:, 0:NR]), rd=[rps], wr=[rbc])
        if with_out:
            nMh, nMl, rnMhl = hilo(negM[:], rnM, [NR, 128], "nM")
            OH = sb("OH", [NR, NR, 128], BF16); rOH = R()
            P.op("dve", lambda e: e.tensor_copy(out=OH[:], in_=C.ident[0:NR, 0:NR].unsqueeze(2).to_broadcast([NR, NR, 128])), rd=[C.rid], wr=[rOH])
            tri = sb("tri", [128, 128]); rtri = R()
            P.op("pool", lambda e: e.memset(tri[:], 1.0), wr=[rtri])
            P.op("pool", lambda e: e.affine_select(out=tri[:], in_=tri[:], compare_op=ALU.is_ge, fill=0.0, base=0,
                                                   pattern=[[1, 128]], channel_multiplier=-1), rd=[rtri], wr=[rtri])
        Cb = sb("Cb", [128, NH, 4, DH], BF16); rCb = [R() for _ in range(NH)]
        nb = sb("nb", [128, NH, 4], BF16); rnb = R()
        rCh = [R() for _ in range(NH)]
        for h in range(NH):
            P.op("act", lambda e, h=h: e.activation(out=Cb[:, h], in_=Ct[:, h], func=AF.Copy), rd=[rC], wr=[rCb[h]])
        P.op("act", lambda e: e.activation(out=nb[:], in_=nt[:], func=AF.Copy), rd=[rn], wr=[rnb])
        kring = Ring(C, ph, "ck", [128, MI], BF16, 2)
        vring = Ring(C, ph, "cv", [128, MI], BF16, 2)
        qTr = Ring(C, ph, "cq", [128, 16, 128], BF16, 2)
        kTr = Ring(C, ph, "ckT", [128, 16, 128], BF16, 2)
        dtr = Ring(C, ph, "cdt", [128, 128], F32, 2)
        scr_ = Ring(C, ph, "csc", [128, 128], F32, 2)
        str_ = Ring(C, ph, "cst", [128, 128], BF16, 2)
        qsr = Ring(C, ph, "cqs", [128, 4, 128], BF16, 2)
        kwr = Ring(C, ph, "ckw", [128, DH], BF16, 2)
        denr = Ring(C, ph, "cden", [128, 1], F32, 2)
        hor = Ring(C, ph, "cho", [128, DH], F32, 2)
        QTv = QKVT[0:MI].rearrange("(j p) t -> p j t", p=128)
        KTv = QKVT[MI:2 * MI].rearrange("(j p) t -> p j t", p=128)
        SCL = DH ** -0.5
        for c in range(NCH):
            kt_, rkt = kring.next()
            P.dma("sp", kt_[:], Ktm[c * 128:(c + 1) * 128, :], wr=[rkt])
            vt_, rvt = vring.next()
            P.dma("sp", vt_[:], Vtm[c * 128:(c + 1) * 128, :], wr=[rvt])
            if with_out:
                qT, rqTl = qTr.next4()
                kT, rkTl = kTr.next4()
                for hh in range(NH):
                    P.dma("sp", qT[:, hh * 4:hh * 4 + 4, :], QTv[:, hh * 4:hh * 4 + 4, c * 128:(c + 1) * 128], wr=[rqTl[hh]])
                    P.dma("sp", kT[:, hh * 4:hh * 4 + 4, :], KTv[:, hh * 4:hh * 4 + 4, c * 128:(c + 1) * 128], wr=[rkTl[hh]])
            for h in range(NH):
                r = h * NCH + c
                if with_out:
                    psb, rpsb = C.psum()
                    P.op("pe", lambda e, psb=psb, r=r: e.matmul(psb[:, 0:128], lhsT=OH[:, r, :], rhs=nMh[:], start=True, stop=False), rd=[rOH, rnMhl], wr=[rpsb])
                    P.op("pe", lambda e, psb=psb, r=r: e.matmul(psb[:, 0:128], lhsT=OH[:, r, :], rhs=nMl[:], start=False, stop=True), rd=[rOH, rnMhl], wr=[rpsb])
                    dt_, rdt = dtr.next()
                    P.op("act", lambda e, psb=psb, dt_=dt_, r=r: e.activation(out=dt_[:], in_=psb[:, 0:128], func=AF.Exp, bias=tmv[:, 0, r:r + 1]),
                         rd=[rpsb, rtmv], wr=[rdt])
                    sc_, rsc = scr_.next()
                    P.op("act", lambda e, psb=psb, sc_=sc_, r=r: e.activation(out=sc_[:], in_=psb[:, 0:128], func=AF.Exp, bias=bc[:, 1, r:r + 1]),
                         rd=[rpsb, rbc], wr=[rsc])
                    P.op("pool", lambda e, dt_=dt_: e.tensor_tensor(out=dt_[:], in0=dt_[:], in1=tri[:], op=ALU.mult), rd=[rdt, rtri], wr=[rdt])
                    pss, rpss = C.psum()
                    for kk in range(4):
                        P.op("pe", lambda e, pss=pss, kT=kT, qT=qT, h=h, kk=kk: e.matmul(
                            pss[:, 0:128], lhsT=kT[:, h * 4 + kk, :], rhs=qT[:, h * 4 + kk, :], start=(kk == 0), stop=(kk == 3)),
                            rd=[rkTl[h], rqTl[h]], wr=[rpss])
                    st_, rst = str_.next()
                    P.op("dve", lambda e, pss=pss, st_=st_, dt_=dt_: e.scalar_tensor_tensor(
                        out=st_[:], in0=pss[:, 0:128], scalar=SCL, in1=dt_[:], op0=ALU.mult, op1=ALU.mult), rd=[rpss, rdt], wr=[rst])
                    qs, rqs = qsr.next()
                    P.op("dve", lambda e, qs=qs, qT=qT, sc_=sc_, h=h: e.scalar_tensor_tensor(
                        out=qs[:], in0=qT[:, h * 4:h * 4 + 4, :], scalar=SCL, in1=sc_[:].unsqueeze(1).to_broadcast([128, 4, 128]),
                        op0=ALU.mult, op1=ALU.mult), rd=[rqTl[h], rsc], wr=[rqs])
                    psn, rpsn = C.psum()
                    P.op("pe", lambda e, psn=psn, st_=st_, vt_=vt_, h=h: e.matmul(
                        psn[:, :], lhsT=st_[:], rhs=vt_[:, h * DH:(h + 1) * DH], start=True, stop=False), rd=[rst, rvt], wr=[rpsn])
                    for kk in range(4):
                        P.op("pe", lambda e, psn=psn, qs=qs, h=h, kk=kk: e.matmul(
                            psn[:, :], lhsT=qs[:, kk, :], rhs=Cb[:, h, kk, :], start=False, stop=(kk == 3)), rd=[rqs, rCb[h]], wr=[rpsn])
                    psd, rpsd = C.psum()
                    P.op("pe", lambda e, psd=psd, st_=st_: e.matmul(psd[:, 0:1], lhsT=st_[:], rhs=onesb[:, 0:1], start=True, stop=False),
                         rd=[rst, rones], wr=[rpsd])
                    for kk in range(4):
                        P.op("pe", lambda e, psd=psd, qs=qs, h=h, kk=kk: e.matmul(
                            psd[:, 0:1], lhsT=qs[:, kk, :], rhs=nb[:, h, kk:kk + 1], start=False, stop=(kk == 3)), rd=[rqs, rnb], wr=[rpsd])
                    den, rden = denr.next()
                    P.op("act", lambda e, psd=psd, den=den: e.activation(out=den[:], in_=psd[:, 0:1], func=AF.Abs), rd=[rpsd], wr=[rden])
                    P.op("dve", lambda e, den=den, r=r: e.tensor_scalar_max(out=den[:], in0=den[:], scalar1=tmv[:, 2, r:r + 1]),
                         rd=[rden, rtmv], wr=[rden])
                    P.op("dve", lambda e, den=den: e.reciprocal(out=den[:], in_=den[:]), rd=[rden], wr=[rden])
                    ho, rho = hor.next()
                    P.op("act", lambda e, psn=psn, ho=ho, den=den: e.activation(out=ho[:], in_=psn[:, :], func=AF.Copy, scale=den[:, 0:1]),
                         rd=[rpsn, rden], wr=[rho])
                    P.dma("sp", Htm[c * 128:(c + 1) * 128, h * DH:(h + 1) * DH], ho[:], rd=[rho])
                kw, rkw = kwr.next()
                P.op("pool", lambda e, kw=kw, kt_=kt_, h=h, r=r: e.tensor_scalar_mul(
                    out=kw[:], in0=kt_[:, h * DH:(h + 1) * DH], scalar1=tmv[:, 1, r:r + 1]), rd=[rkt, rtmv], wr=[rkw])
                for kk in range(4):
                    psc, rpsc = C.psum()
                    P.op("pe", lambda e, psc=psc, kw=kw, vt_=vt_, h=h, kk=kk: e.matmul(
                        psc[:, :], lhsT=kw[:, kk * 128:(kk + 1) * 128], rhs=vt_[:, h * DH:(h + 1) * DH], start=True, stop=True),
                        rd=[rkw, rvt], wr=[rpsc])
                    P.op("dve", lambda e, psc=psc, h=h, kk=kk, r=r: e.scalar_tensor_tensor(
                        out=Ct[:, h, kk, :], in0=Ct[:, h, kk, :], scalar=bc[:, 0, r:r + 1], in1=psc[:, :], op0=ALU.mult, op1=ALU.add),
                        rd=[rC, rbc, rpsc, rCb[h]], wr=[rC])
                    P.op("act", lambda e, h=h, kk=kk: e.activation(out=Cb[:, h, kk, :], in_=Ct[:, h, kk, :], func=AF.Copy), rd=[rC], wr=[rCb[h]])
                psm, rpsm = C.psum()
                for kk in range(4):
                    P.op("pe", lambda e, psm=psm, kw=kw, kk=kk: e.matmul(psm[:, kk:kk + 1], lhsT=kw[:, kk * 128:(kk + 1) * 128], rhs=onesb[:, 0:1],
                                                                       start=True, stop=True), rd=[rkw, rones], wr=[rpsm])
                P.op("dve", lambda e, psm=psm, h=h, r=r: e.scalar_tensor_tensor(
                    out=nt[:, h, :], in0=nt[:, h, :], scalar=bc[:, 0, r:r + 1], in1=psm[:, 0:4], op0=ALU.mult, op1=ALU.add),
                    rd=[rn, rbc, rpsm, rnb], wr=[rn])
                P.op("act", lambda e, h=h: e.activation(out=nb[:, h, :], in_=nt[:, h, :], func=AF.Copy), rd=[rn], wr=[rnb])
        if outs is not None:
            Cst, nst, FG = outs
            P.dma("sp", Cst.rearrange("h (kt p) v -> p h kt v", p=128), Ct[:], rd=[rC], is_output=True)
            P.dma("sp", nst.rearrange("h (kt p) -> p h kt", p=128), nt[:], rd=[rn], is_output=True, slow=True)
            P.dma("sp", FG, fg[:].rearrange("p a h -> p (a h)"), rd=[rfg], is_output=True)
        P.barrier()


def ml_scratch(nc):
    dt = lambda n, s, d: nc.dram_tensor(n, s, d).ap()
    return (dt("XM", [MI, TE], F32), dt("XC", [MI, T], BF16), dt("XMo", [MI, T], BF16), dt("QKVT", [3 * MI, T], BF16),
            dt("Ktm", [T, MI], BF16), dt("Vtm", [T, MI], BF16), dt("GTfm", [8, T], F32))


def ml_inputs_decl(nc):
    din = lambda n, s: nc.dram_tensor(n, s, F32, kind="ExternalInput").ap()
    return dict(x1Te=din("x1Te", [D, TE]), w_in=din("ml_w_in", [D, 2 * MI]), convw=din("convw", [128, 16, 4]),
                convb=din("convb", [128, 16]), wq=din("wq_bd", [16, 128, 128]), wk=din("wk_bd", [16, 128, 128]),
                wv=din("wv_bd", [16, 128, 128]), wgt=din("ml_w_gates", [3 * MI, 8]), bgt=din("ml_b_gates", [8, 1]))


def state_tiles(C, st):
    nc, P = C.nc, C.P
    Ct = st.enter_context(nc.sbuf_tensor("Cst_sb", [128, NH, 4, DH], F32)); rC = R()
    nt = st.enter_context(nc.sbuf_tensor("nst_sb", [128, NH, 4], F32)); rn = R()
    m0 = st.enter_context(nc.sbuf_tensor("m0_sb", [128, NH], F32)); rm0 = R()
    P.op("pool", lambda e: e.memset(Ct[:], 0.0), wr=[rC])
    P.op("pool", lambda e: e.memset(nt[:], 0.0), wr=[rn])
    return (Ct, rC), (nt, rn), (m0, rm0)


def build_l1a():
    nc = bass.Bass("TRN2", target_bir_lowering=False)
    a = ml_inputs_decl(nc)
    Cst = nc.dram_tensor("Cst", [NH, DH, DH], F32, kind="ExternalOutput").ap()
    nst = nc.dram_tensor("nst", [NH, DH], F32, kind="ExternalOutput").ap()
    FG = nc.dram_tensor("FG", [1, 2 * NH], F32, kind="ExternalOutput").ap()
    scr = ml_scratch(nc)
    with ExitStack() as st:
        C = Ctx(nc, st)
        Cs, ns, m0s = state_tiles(C, st)
        C.P.op("pool", lambda e: e.memset(m0s[0][:], NEG), wr=[m0s[1]])
        ml_front(C, a["x1Te"], a["w_in"], a["convw"], a["convb"], a["wq"], a["wk"], a["wv"], a["wgt"], a["bgt"], scr)
        ml_cell(C, scr, m0s, Cs, ns, False, None, (Cst, nst, FG))
        C.P.finish()
    return nc


def build_l1b():
    nc = bass.Bass("TRN2", target_bir_lowering=False)
    a = ml_inputs_decl(nc)
    din = lambda n, s: nc.dram_tensor(n, s, F32, kind="ExternalInput").ap()
    Cin = din("Cin", [3, NH, DH, DH]); nin = din("nin", [3, NH, DH]); FGin = din("FGin", [3, 2 * NH])
    x1tm = din("x1tm", [T, D]); pT = din("pT", [PLE, T]); skip = din("skipc", [128, 16]); normg = din("ml_norm_g", [NH, DH])
    w_out = din("ml_w_out", [MI, D])
    ln1g = din("ln1_g", [D]); ln1b = din("ln1_b", [D]); ln2g = din("ln2_g", [D]); ln2b = din("ln2_b", [D])
    wg = din("ffn_w_gate", [D, DFF]); wu = din("ffn_w_up", [D, DFF]); wd = din("ffn_w_down", [DFF, D])
    pwp = din("ple_w_proj", [PLE, D]); pwg = din("ple_w_gate", [D, D]); pbg = din("ple_b_gate", [D])
    out = nc.dram_tensor("out", [T, D], F32, kind="ExternalOutput").ap()
    dt = lambda n, s, d: nc.dram_tensor(n, s, d).ap()
    Htm = dt("Htm", [T, MI], F32); HNT = dt("HNT", [MI, T], BF16); HG = dt("HG", [MI, T], BF16)
    scr = ml_scratch(nc)
    tscr = tail_scratch(nc)
    with ExitStack() as st:
        C = Ctx(nc, st)
        P = C.P
        (Ct, rC), (nt, rn), (m0, rm0) = state_tiles(C, st)
        P.op("pool", lambda e: e.memset(m0[:], 0.0), wr=[rm0])
        with ExitStack() as ph:
            sb = lambda name, shape: ph.enter_context(nc.sbuf_tensor(C.name(name), shape, F32))
            tC = sb("tC", [128, NH, 4, DH]); rtC = R()
            tn = sb("tn", [128, NH, 4]); rtn = R()
            for s in range(3):
                Fb = sb("Fb", [128, NH]); Gb = sb("Gb", [128, NH]); rF = R(); rG = R()
                P.dma("sp", Fb[:], FGin[s, 0:NH].partition_broadcast(128), wr=[rF])
                P.dma("sp", Gb[:], FGin[s, NH:2 * NH].partition_broadcast(128), wr=[rG])
                P.dma("sp", tC[:], Cin[s].rearrange("h (kt p) v -> p h kt v", p=128), wr=[rtC])
                P.dma("sp", tn[:], nin[s].rearrange("h (kt p) -> p h kt", p=128), wr=[rtn], slow=True)
                tt = sb("tt", [128, NH]); mn = sb("mn", [128, NH]); a1 = sb("a1", [128, NH]); a2 = sb("a2", [128, NH])
                rtt, rmn, ra1, ra2 = R(), R(), R(), R()
                P.op("dve", lambda e, tt=tt, Fb=Fb: e.tensor_add(out=tt[:], in0=Fb[:], in1=m0[:]), rd=[rF, rm0], wr=[rtt])
                P.op("dve", lambda e, mn=mn, tt=tt, Gb=Gb: e.tensor_max(out=mn[:], in0=tt[:], in1=Gb[:]), rd=[rtt, rG], wr=[rmn])
                P.op("dve", lambda e, a1=a1, tt=tt, mn=mn: e.tensor_sub(out=a1[:], in0=tt[:], in1=mn[:]), rd=[rtt, rmn], wr=[ra1])
                P.op("act", lambda e, a1=a1: e.activation(out=a1[:], in_=a1[:], func=AF.Exp), rd=[ra1], wr=[ra1])
                P.op("dve", lambda e, a2=a2, Gb=Gb, mn=mn: e.tensor_sub(out=a2[:], in0=Gb[:], in1=mn[:]), rd=[rG, rmn], wr=[ra2])
                P.op("act", lambda e, a2=a2: e.activation(out=a2[:], in_=a2[:], func=AF.Exp), rd=[ra2], wr=[ra2])
                for h in range(NH):
                    P.op("dve", lambda e, h=h, a1=a1: e.tensor_scalar_mul(out=Ct[:, h], in0=Ct[:, h], scalar1=a1[:, h:h + 1]),
                         rd=[rC, ra1], wr=[rC])
                    P.op("dve", lambda e, h=h, a2=a2: e.scalar_tensor_tensor(out=Ct[:, h], in0=tC[:, h], scalar=a2[:, h:h + 1], in1=Ct[:, h],
                                                                            op0=ALU.mult, op1=ALU.add), rd=[rtC, ra2, rC], wr=[rC])
                    P.op("dve", lambda e, h=h, a1=a1: e.tensor_scalar_mul(out=nt[:, h], in0=nt[:, h], scalar1=a1[:, h:h + 1]),
                         rd=[rn, ra1], wr=[rn])
                    P.op("dve", lambda e, h=h, a2=a2: e.scalar_tensor_tensor(out=nt[:, h], in0=tn[:, h], scalar=a2[:, h:h + 1], in1=nt[:, h],
                                                                            op0=ALU.mult, op1=ALU.add), rd=[rtn, ra2, rn], wr=[rn])
                P.op("dve", lambda e, mn=mn: e.tensor_copy(out=m0[:], in_=mn[:]), rd=[rmn, rtt], wr=[rm0])
            P.barrier()
        ml_front(C, a["x1Te"], a["w_in"], a["convw"], a["convb"], a["wq"], a["wk"], a["wv"], a["wgt"], a["bgt"], scr)
        ml_cell(C, scr, (m0, rm0), (Ct, rC), (nt, rn), True, Htm, None)
        for h in range(NH):
            ln_tm(C, Htm[:, h * DH:(h + 1) * DH], normg[h], None, DH, None, HNT[h * DH:(h + 1) * DH, :])
        XC = scr[1]
        with ExitStack() as ph:
            sk = ph.enter_context(nc.sbuf_tensor(C.name("skip"), [128, 16], F32)); rsk = R()
            P.dma("sp", sk[:], skip, wr=[rsk])
            szr = Ring(C, ph, "sz", [128, 512], F32, 3)
            hnr = Ring(C, ph, "hnb", [128, 512], BF16, 3)
            xcr = Ring(C, ph, "xcb", [128, 512], BF16, 3)
            tr = Ring(C, ph, "hgt", [128, 512], F32, 3)
            yr = Ring(C, ph, "hgo", [128, 512], BF16, 3)

            def epi(ps, rps, ot, t0, tw):
                sz, rsz = szr.next()
                P.op("act", lambda e: e.activation(out=sz[:, :tw], in_=ps[:, :tw], func=AF.Silu), rd=[rps], wr=[rsz])
                hn, rhn = hnr.next()
                P.dma("sp", hn[:, :tw], HNT[ot * 128:(ot + 1) * 128, t0:t0 + tw], wr=[rhn])
                xc, rxc = xcr.next()
                P.dma("sp", xc[:, :tw], XC[ot * 128:(ot + 1) * 128, t0:t0 + tw], wr=[rxc])
                t, rt = tr.next()
                P.op("dve", lambda e: e.scalar_tensor_tensor(out=t[:, :tw], in0=xc[:, :tw], scalar=sk[:, ot:ot + 1], in1=hn[:, :tw],
                                                             op0=ALU.mult, op1=ALU.add), rd=[rxc, rsk, rhn], wr=[rt])
                y, ry = yr.next()
                P.op("dve", lambda e: e.tensor_tensor(out=y[:, :tw], in0=t[:, :tw], in1=sz[:, :tw], op=ALU.mult), rd=[rt, rsz], wr=[ry])
                P.dma("sp", HG[ot * 128:(ot + 1) * 128, t0:t0 + tw], y[:, :tw], rd=[ry])
            linear(C, a["x1Te"][:, 128:128 + T], a["w_in"][:, MI:2 * MI], D, MI, T, "fm", epi)
        layer_tail(C, None, x1tm, HG, MI, w_out, ln1g, ln1b, wg, wu, wd, ln2g, ln2b, pwg, pbg, pT, pwp, out, tscr)
        P.finish()
    return nc


def _bd(w):
    o = np.zeros((16, 128, 128), np.float32)
    w = np.asarray(w, np.float32).reshape(16, 32, 4, 4)
    for n in range(32):
        o[:, 4 * n:4 * n + 4, 4 * n:4 * n + 4] = w[:, n]
    return o


def _cols(v):
    return np.ascontiguousarray(np.asarray(v, np.float32).reshape(16, 128).T)


def ml_common_inputs(inp, x1Te):
    cw = np.asarray(inp["ml_conv_w"][0], np.float32)
    return {"x1Te": x1Te, "ml_w_in": np.ascontiguousarray(inp["ml_w_in"][0]),
            "convw": np.ascontiguousarray(cw.reshape(4, 16, 128).transpose(2, 1, 0)), "convb": _cols(inp["ml_conv_b"][0]),
            "wq_bd": _bd(inp["ml_w_q"][0]), "wk_bd": _bd(inp["ml_w_k"][0]), "wv_bd": _bd(inp["ml_w_v"][0]),
            "ml_w_gates": np.ascontiguousarray(inp["ml_w_gates"][0]),
            "ml_b_gates": np.ascontiguousarray(np.asarray(inp["ml_b_gates"][0], np.float32).reshape(8, 1))}


def make_x1Te(x1, c):
    b, j = c // 4, c % 4
    own = x1[b, j * T:(j + 1) * T]
    prev = x1[b, j * T - 128:j * T] if j > 0 else np.zeros((128, D), np.float32)
    return np.ascontiguousarray(np.concatenate([prev, own], axis=0).T)


def kernel(**inp):
    inp = {k: np.asarray(v) for k, v in inp.items()}
    cores = list(range(NCORES))
    o0 = run_l0(inp)
    x1 = np.stack([np.concatenate(o0[b * 4:(b + 1) * 4], axis=0) for b in range(2)]).astype(np.float32)
    x1Te = [make_x1Te(x1, c) for c in cores]
    ra = run_bass_kernel_spmd(build_l1a(), [ml_common_inputs(inp, x1Te[c]) for c in cores], core_ids=cores).results
    maps = []
    for c in cores:
        b, j = c // 4, c % 4
        Cin = np.zeros((3, NH, DH, DH), np.float32); nin = np.zeros((3, NH, DH), np.float32)
        FGin = np.zeros((3, 2 * NH), np.float32); FGin[:, NH:] = NEG
        for s in range(j):
            src = ra[b * 4 + s]
            slot = 3 - j + s
            Cin[slot] = src["Cst"]; nin[slot] = src["nst"]; FGin[slot] = np.asarray(src["FG"]).reshape(-1)
        m = ml_common_inputs(inp, x1Te[c])
        m.update({"Cin": Cin, "nin": nin, "FGin": FGin, "x1tm": np.ascontiguousarray(x1[b, j * T:(j + 1) * T]),
                  "pT": np.ascontiguousarray(inp["p"][1, b, j * T:(j + 1) * T].T), "skipc": _cols(inp["ml_skip"][0]),
                  "ml_norm_g": np.ascontiguousarray(inp["ml_norm_g"][0]), "ml_w_out": np.ascontiguousarray(inp["ml_w_out"][0])})
        for k in ("ln1_g", "ln1_b", "ln2_g", "ln2_b", "ffn_w_gate", "ffn_w_up", "ffn_w_down", "ple_w_proj", "ple_w_gate", "ple_b_gate"):
            m[k] = np.ascontiguousarray(inp[k][1])
        maps.append(m)
    rb = run_bass_kernel_spmd(build_l1b(), maps, core_ids=cores).results
    out = np.stack([np.concatenate([rb[b * 4 + j]["out"] for j in range(4)], axis=0) for b in range(2)])
    return out.astype(np.float32)
```
